# Optimizing a Trainium2 kernel written in Bass

```python
import math
import jax, jax.numpy as jnp
from jax import lax
import numpy as np

D_MODEL = 1024
BATCH = 2
SEQ = 16384
DEPTH = 2
DEC_BATCH = 8
DEC_SEQ = 4096
PAST_LEN = 128

MIX_WIDTH = D_MODEL
POOL_WIDTH = MIX_WIDTH // 2
ATTN_WIDTH = MIX_WIDTH - POOL_WIDTH
POOL_WINDOWS = (2, 4, 8, 16)
N_POOL_GROUPS = len(POOL_WINDOWS)
POOL_GROUP_DIM = POOL_WIDTH // N_POOL_GROUPS
DIFF_HEAD_DIM = 64
N_HEADS = ATTN_WIDTH // (2 * DIFF_HEAD_DIM)
V_HEAD_DIM = 2 * DIFF_HEAD_DIM
ROPE_THETA = 10000.0
Q_BLOCK = 128
NORM_EPS = 1e-6
SUBLN_EPS = 1e-5
IN_WIDTH = 2 * POOL_WIDTH + 4 * ATTN_WIDTH
SPLIT_POINTS = (POOL_WIDTH, 2 * POOL_WIDTH, 2 * POOL_WIDTH + ATTN_WIDTH,
                2 * POOL_WIDTH + 2 * ATTN_WIDTH, 2 * POOL_WIDTH + 3 * ATTN_WIDTH)

kernel_name = "hybrid_pool_diffattn_encoder"


def rms_norm(x, g, eps=NORM_EPS):
    xf = x.astype(jnp.float32)
    y = xf * lax.rsqrt(jnp.mean(xf * xf, axis=-1, keepdims=True) + eps)
    return (y * g.astype(jnp.float32)).astype(x.dtype)


def rope(x, S):
    dh = x.shape[-1]
    pos = jnp.arange(S, dtype=jnp.float32)
    inv_freq = ROPE_THETA ** (-jnp.arange(0, dh, 2, dtype=jnp.float32) / dh)
    ang = pos[:, None] * inv_freq[None, :]
    cos = jnp.concatenate([jnp.cos(ang), jnp.cos(ang)], -1)[None, :, None, None, :].astype(x.dtype)
    sin = jnp.concatenate([jnp.sin(ang), jnp.sin(ang)], -1)[None, :, None, None, :].astype(x.dtype)
    x1, x2 = jnp.split(x, 2, axis=-1)
    return x * cos + jnp.concatenate([-x2, x1], axis=-1) * sin


def multiscale_pool(u, pool_w, pool_scale):
    B, S, _ = u.shape
    uf = u.astype(jnp.float32).reshape(B, S, N_POOL_GROUPS, POOL_GROUP_DIM)
    cs = jnp.concatenate([jnp.zeros((B, 1, N_POOL_GROUPS, POOL_GROUP_DIM), jnp.float32),
                          jnp.cumsum(uf, axis=1)], axis=1)
    t = jnp.arange(S)
    means = []
    for g, w in enumerate(POOL_WINDOWS):
        lo = jnp.clip(t - w // 2, 0, S)
        hi = jnp.clip(t - w // 2 + w, 0, S)
        csg = cs[:, :, g]
        s = jnp.take(csg, hi, axis=1) - jnp.take(csg, lo, axis=1)
        means.append(s / (hi - lo).astype(jnp.float32)[None, :, None])
    diff = (jnp.stack(means, axis=2) - uf).astype(u.dtype)
    out = jnp.einsum('bsgc,gcd->bsgd', diff, pool_w).reshape(B, S, POOL_WIDTH)
    return out * pool_scale


def diff_attention(q, k, v, lam, subln_g, lambda_init):
    B, S, H, _, dh = q.shape
    nblk = S // Q_BLOCK
    qb = q.reshape(B, nblk, Q_BLOCK, H, 2, dh).transpose(1, 0, 2, 3, 4, 5)
    scale = dh ** -0.5

    def block(qblk):
        s = jnp.einsum('bqhcd,bkhcd->bhcqk', qblk, k,
                       preferred_element_type=jnp.float32) * scale
        p = jax.nn.softmax(s, axis=-1)
        a = p[:, :, 0] - lam * p[:, :, 1]
        return jnp.einsum('bhqk,bkhe->bqhe', a.astype(v.dtype), v)

    o = lax.map(block, qb)
    o = o.transpose(1, 0, 2, 3, 4).reshape(B, S, H, V_HEAD_DIM)
    o = rms_norm(o, subln_g, SUBLN_EPS) * (1.0 - lambda_init)
    return o.reshape(B, S, ATTN_WIDTH)


def layer(x, l, pre_g, w_in, pool_w, pool_scale, lq1, lk1, lq2, lk2, subln_g, w_out, post_g):
    B, S, _ = x.shape
    h = rms_norm(x, pre_g)
    proj = h @ w_in
    u, zp, q, k, v, za = jnp.split(proj, SPLIT_POINTS, axis=-1)
    pool_out = multiscale_pool(u, pool_w, pool_scale) * jax.nn.silu(zp)
    q = rope(q.reshape(B, S, N_HEADS, 2, DIFF_HEAD_DIM), S)
    k = rope(k.reshape(B, S, N_HEADS, 2, DIFF_HEAD_DIM), S)
    v = v.reshape(B, S, N_HEADS, V_HEAD_DIM)
    lambda_init = 0.8 - 0.6 * math.exp(-0.3 * l)
    lam = (jnp.exp(jnp.sum(lq1.astype(jnp.float32) * lk1.astype(jnp.float32)))
           - jnp.exp(jnp.sum(lq2.astype(jnp.float32) * lk2.astype(jnp.float32)))
           + lambda_init)
    attn_out = diff_attention(q, k, v, lam, subln_g, lambda_init) * jax.nn.silu(za)
    y = jnp.concatenate([pool_out, attn_out], axis=-1) @ w_out
    return x + rms_norm(y, post_g)


def trunk(x, pre_norm_g, w_in, pool_w, pool_scale, lambda_q1, lambda_k1,
          lambda_q2, lambda_k2, subln_g, w_out, post_norm_g):
    for l in range(DEPTH):
        x = layer(x, l, pre_norm_g[l], w_in[l], pool_w[l], pool_scale[l],
                  lambda_q1[l], lambda_k1[l], lambda_q2[l], lambda_k2[l],
                  subln_g[l], w_out[l], post_norm_g[l])
    return x


def setup_inputs(seed: int = 0) -> dict:
    key = jax.random.key(seed)
    ks = jax.random.split(key, 14)
    f32 = jnp.float32
    return {
        "x_prompt": jax.random.normal(ks[0], (BATCH, SEQ, D_MODEL), f32),
        "x_sample": jax.random.normal(ks[1], (DEC_BATCH, DEC_SEQ, D_MODEL), f32),
        "pre_norm_g": 1.0 + 0.02 * jax.random.normal(ks[2], (DEPTH, D_MODEL), f32),
        "w_in": jax.random.normal(ks[3], (DEPTH, D_MODEL, IN_WIDTH), f32) * D_MODEL ** -0.5,
        "pool_w": jax.random.normal(ks[4], (DEPTH, N_POOL_GROUPS, POOL_GROUP_DIM, POOL_GROUP_DIM), f32) * POOL_GROUP_DIM ** -0.5,
        "pool_scale": 1.0 + 0.02 * jax.random.normal(ks[5], (DEPTH, POOL_WIDTH), f32),
        "lambda_q1": 0.1 * jax.random.normal(ks[6], (DEPTH, DIFF_HEAD_DIM), f32),
        "lambda_k1": 0.1 * jax.random.normal(ks[7], (DEPTH, DIFF_HEAD_DIM), f32),
        "lambda_q2": 0.1 * jax.random.normal(ks[8], (DEPTH, DIFF_HEAD_DIM), f32),
        "lambda_k2": 0.1 * jax.random.normal(ks[9], (DEPTH, DIFF_HEAD_DIM), f32),
        "subln_g": 1.0 + 0.02 * jax.random.normal(ks[10], (DEPTH, V_HEAD_DIM), f32),
        "w_out": jax.random.normal(ks[11], (DEPTH, MIX_WIDTH, D_MODEL), f32) * MIX_WIDTH ** -0.5,
        "post_norm_g": 1.0 + 0.02 * jax.random.normal(ks[12], (DEPTH, D_MODEL), f32),
    }


def reference(x_prompt, x_sample, pre_norm_g, w_in, pool_w, pool_scale, lambda_q1, lambda_k1,
              lambda_q2, lambda_k2, subln_g, w_out, post_norm_g):
    y_prompt = trunk(x_prompt, pre_norm_g, w_in, pool_w, pool_scale, lambda_q1, lambda_k1,
                     lambda_q2, lambda_k2, subln_g, w_out, post_norm_g)
    y_sample = trunk(x_sample, pre_norm_g, w_in, pool_w, pool_scale, lambda_q1, lambda_k1,
                     lambda_q2, lambda_k2, subln_g, w_out, post_norm_g)
    return (y_prompt, y_sample)
```

```python
import math
import os
from contextlib import ExitStack

import numpy as np
import ml_dtypes

import concourse.bass as bass
import concourse.mybir as mybir
from concourse.bass_utils import run_bass_kernel_spmd

F32 = mybir.dt.float32
BF16 = mybir.dt.bfloat16
I32 = mybir.dt.int32
AF = mybir.ActivationFunctionType
ALU = mybir.AluOpType
AX = mybir.AxisListType

NCORES = 8
D = 1024
T = 4096
NSEQ = 2
NRANK = 4
CH = 512
NCH = T // CH
VW = 132
HV = 32 * VW
DEPTH = 2
NORM_EPS = 1e-6
SUBLN_EPS = 1e-5
THETA = 10000.0
WINDOWS = (2, 4, 8, 16)
PI = math.pi

ENGS = ("pe", "act", "dve", "pool", "sp")
DMA_SEMS = ("ld", "st", "kv0", "kv1", "kv2", "kv3", "qld", "xin", "stg", "wld", "tab", "ost", "pld")


class Sem:
    def __init__(self, h, step=1):
        self.h = h
        self.n = 0
        self.step = step


class Prog:
    def __init__(self):
        self.q = {e: [] for e in ENGS}
        self.waited = {}

    def op(self, eng, fn, inc=None):
        if inc is not None:
            inc.n += inc.step
            self.q[eng].append(lambda e, fn=fn, s=inc: fn(e).then_inc(s.h, s.step))
            return inc.n
        self.q[eng].append(fn)
        return None

    def wait(self, eng, sem, val):
        if val is None or val <= 0:
            return
        key = (eng, id(sem))
        if self.waited.get(key, 0) >= val:
            return
        self.waited[key] = val
        self.q[eng].append(lambda e, s=sem, v=val: e.wait_ge(s.h, v))


def build(debug=False, nlayers=DEPTH, stop_after=None):
    nc = bass.Bass("TRN2", target_bir_lowering=False)
    es = ExitStack()
    P = Prog()

    def dram(name, shape, dt, kind="Internal"):
        return nc.dram_tensor(name, shape, dt, kind=kind).ap()

    dbg_kind = "ExternalOutput" if debug else "Internal"

    x_in = dram("x_in", [NSEQ * T, D], F32, "ExternalInput")
    w_in = dram("w_in", [DEPTH, D, 3072], F32, "ExternalInput")
    w_out = dram("w_out", [DEPTH, D, D], F32, "ExternalInput")
    pool_w = dram("pool_w", [DEPTH, 4, 128, 128], F32, "ExternalInput")
    pool_scale = dram("pool_scale", [DEPTH, 512], F32, "ExternalInput")
    pre_g = dram("pre_g", [DEPTH, D], F32, "ExternalInput")
    post_g = dram("post_g", [DEPTH, D], F32, "ExternalInput")
    subln_g = dram("subln_g", [DEPTH, 128], F32, "ExternalInput")
    lam_in = dram("lam_in", [DEPTH, 4 * 64], F32, "ExternalInput")
    meta = dram("meta", [128, 16], F32, "ExternalInput")
    y_out = dram("y_out", [NSEQ * T, D], F32, "ExternalOutput")

    x1 = dram("x1", [NSEQ * T, D], F32)
    cosT = dram("cosT", [NSEQ * 128, T], F32, dbg_kind)
    sinT = dram("sinT", [NSEQ * 128, T], F32, dbg_kind)
    qT_d = dram("qT_d", [NSEQ * 512, T], BF16, dbg_kind)
    kTA_d = dram("kTA_d", [512, T], BF16, dbg_kind)
    vA_d = dram("vA_d", [128, 4 * HV], BF16, dbg_kind)
    kTl_h = [dram("kTl_%d" % h, [128, T], BF16) for h in range(4)]
    kTall_h = [dram("kTall_%d" % h, [NRANK * 128, T], BF16) for h in range(4)]
    vl_h = [[dram("vl_%d_%d" % (h, f), [128, 16 * VW], BF16) for f in range(2)] for h in range(4)]
    vall_h = [[dram("vall_%d_%d" % (h, f), [NRANK * 128, 16 * VW], BF16) for f in range(2)] for h in range(4)]
    uT_d = dram("uT_d", [NSEQ * 512, T], F32, dbg_kind)
    uel_d = dram("uel_d", [512, 16], F32)
    ueall_d = dram("ueall_d", [NRANK * 512, 16], F32)
    zpT_d = dram("zpT_d", [NSEQ * 512, T], F32, dbg_kind)
    zaT_d = dram("zaT_d", [NSEQ * 512, T], BF16, dbg_kind)
    aoT_d = dram("aoT_d", [NSEQ * 512, T], BF16, dbg_kind)
    dbg_oc = dram("dbg_oc", [128, 1536], F32, dbg_kind)
    dbg_p = dram("dbg_p", [128, 1024], BF16, dbg_kind)

    def sb(name, shape, dt):
        return es.enter_context(nc.sbuf_tensor(name, shape, dt))

    def ps(name, shape, dt=F32):
        return es.enter_context(nc.psum_tensor(name, shape, dt))

    sem_names = list(DMA_SEMS) + [
        "pe", "act", "dve", "pool", "bar", "cc",
        "s_ready", "exp_done", "av_done", "accfree", "tp", "tpc_d", "tpc_a",
        "pj", "pjf_a", "pjf_d", "ep", "epf",
    ]
    S = {}
    for n in sem_names:
        S[n] = Sem(es.enter_context(nc.semaphore(n)), 16 if n in DMA_SEMS else 1)
    SELF = {"act": S["act"], "dve": S["dve"], "pool": S["pool"], "pe": S["pe"]}

    def sop(eng, fn):
        v = P.op(eng, fn, SELF[eng])
        P.wait(eng, SELF[eng], v)
        return v

    def dma(eng, sem, out, in_, slow=False):
        if slow:
            return P.op(eng, lambda e: e.dma_start(out=out, in_=in_, allow_slow_non_contiguous=True), S[sem])
        return P.op(eng, lambda e: e.dma_start(out=out, in_=in_), S[sem])

    def barrier():
        s = S["bar"]
        for e in ENGS:
            for n in sem_names:
                if n != "bar":
                    P.wait(e, S[n], S[n].n)
        base = s.n
        for e in ENGS:
            P.op(e, lambda en: en.nop(), s)
        for e in ENGS:
            P.wait(e, s, base + len(ENGS))

    def wait_all(eng, names):
        for n in names:
            P.wait(eng, S[n], S[n].n)

    ARENA_W = 18816
    arena = sb("arena", [128, ARENA_W], F32)

    class Carver:
        def __init__(self):
            self.off = 0

        def __call__(self, name, shape, dt):
            n = 1
            for d_ in shape[1:]:
                n *= d_
            esz = 4 if dt in (F32, I32) else 2
            words = (n * esz + 3) // 4
            words = (words + 7) // 8 * 8
            assert self.off + words <= ARENA_W, (name, self.off, words)
            ap = arena[:, self.off:self.off + words]
            self.off += words
            if dt != F32:
                ap = ap.bitcast(dt)
            ap = ap[:, 0:n]
            if len(shape) > 2:
                names = "abcd"[:len(shape) - 1]
                kw = {names[i]: shape[1 + i] for i in range(len(shape) - 1)}
                ap = ap.rearrange("p (%s) -> p %s" % (" ".join(names), " ".join(names)), **kw)
            return ap

    class TileV:
        def __init__(self, ap):
            self.ap = ap

        def __getitem__(self, k):
            return self.ap[k]

    BIGN = 16384 + 128 * VW
    big = sb("big", [128, BIGN], BF16)
    Wv = big[:, 0:8 * 4096].rearrange("p (c f) -> p c f", c=8)
    KT = big[:, 0:16384]
    VV = big[:, 16384:BIGN].rearrange("p (k w) -> p k w", w=VW)
    Wo = sb("Wo", [128, 8, D], BF16)
    PW = sb("PW", [128, 4, 128], BF16)
    ident = sb("ident", [128, 128], BF16)
    metas = sb("metas", [128, 16], F32)
    preg = sb("preg", [128, 8], F32)
    npreg = sb("npreg", [128, 8], F32)
    pscale = sb("pscale", [128, 4], F32)
    postg = sb("postg", [128, D], F32)
    sg = sb("sg", [128, 1], F32)
    lamt = sb("lamt", [128, 256], F32)
    lamp = sb("lamp", [128, 128], F32)
    lam = sb("lam", [128, 4], F32)
    negpi = sb("negpi", [128, 1], F32)
    epsn = sb("epsn", [128, 2], F32)
    invL = sb("invL", [128, NSEQ * 4 * 8], F32)
    invR = sb("invR", [128, NSEQ * 4 * 8], F32)
    wstage = sb("wstage", [128, 3072], F32)
    scr = sb("scr", [128, T], F32)
    ss5 = sb("ss5", [128, 8], F32)
    rstd5 = sb("rstd5", [128, 8], F32)

    v_meta = dma("sp", "ld", metas[:], meta)
    cv0 = Carver()
    idi = TileV(cv0("idi", [128, 128], I32))
    idf = TileV(cv0("idf", [128, 128], F32))
    sop("pool", lambda e: e.iota(idi[:], [[1, 128]], base=0, channel_multiplier=-1))
    sop("pool", lambda e: e.tensor_copy(out=idf[:], in_=idi[:]))
    sop("pool", lambda e: e.tensor_single_scalar(out=ident[:], in_=idf[:], scalar=0.0, op=ALU.is_equal))
    sop("pool", lambda e: e.memset(negpi[:], -PI))
    sop("pool", lambda e: e.memset(epsn[:, 0:1], NORM_EPS))
    sop("pool", lambda e: e.memset(epsn[:, 1:2], SUBLN_EPS))

    ti = TileV(cv0("ti", [128, T], I32))
    pi_ = sb("pi_", [128, 1], I32)
    pf = sb("pf", [128, 1], F32)
    invf = sb("invf", [128, 1], F32)
    ang = TileV(cv0("ang", [128, T], F32))
    angsh = TileV(cv0("angsh", [128, T // 2], F32))
    kf = TileV(cv0("kf", [128, T // 2], F32))
    ki = TileV(cv0("ki", [128, T // 2], I32))
    r1 = TileV(cv0("r1", [128, T // 2], F32))
    tabs = TileV(cv0("tabs", [128, T // 2], F32))
    sop("pool", lambda e: e.iota(ti[:], [[1, T]], base=0, channel_multiplier=0))
    sop("pool", lambda e: e.iota(pi_[:], [[0, 1]], base=0, channel_multiplier=1))
    sop("pool", lambda e: e.tensor_copy(out=scr[:, 0:T], in_=ti[:]))
    v_pi = S["pool"].n
    P.wait("dve", S["pool"], v_pi)
    sop("dve", lambda e: e.tensor_single_scalar(out=pi_[:], in_=pi_[:], scalar=31, op=ALU.bitwise_and))
    v_pf = sop("dve", lambda e: e.tensor_copy(out=pf[:], in_=pi_[:]))
    bi = sb("bi", [128, 1], I32)
    bf_ = sb("bf_", [128, 1], F32)
    sop("dve", lambda e: e.memset(invf[:], 1.0))
    for k in range(5):
        ck = THETA ** (-(2.0 ** k) / 32.0)
        sop("dve", lambda e, k=k: e.tensor_scalar(out=bi[:], in0=pi_[:], scalar1=k, scalar2=1,
                                                  op0=ALU.logical_shift_right, op1=ALU.bitwise_and))
        sop("dve", lambda e: e.tensor_copy(out=bf_[:], in_=bi[:]))
        sop("dve", lambda e, ck=ck: e.tensor_scalar(out=bf_[:], in0=bf_[:], scalar1=float(ck - 1.0), scalar2=1.0,
                                                    op0=ALU.mult, op1=ALU.add))
        sop("dve", lambda e: e.tensor_tensor(out=invf[:], in0=invf[:], in1=bf_[:], op=ALU.mult))
    P.wait("dve", S["pool"], S["pool"].n)
    P.wait("dve", S["ld"], v_meta)
    HT = T // 2
    for s in range(NSEQ):
        sop("dve", lambda e, s=s: e.tensor_scalar(out=ang[:], in0=scr[:, 0:T], scalar1=metas[:, s:s + 1],
                                                  scalar2=invf[:, 0:1], op0=ALU.add, op1=ALU.mult))
        for shift, dst in ((0.0, sinT), (0.5 * PI, cosT)):
            for hh in range(2):
                cs_ = slice(hh * HT, (hh + 1) * HT)
                P.wait("dve", S["act"], S["act"].n)
                sop("dve", lambda e, shift=shift, cs_=cs_: e.tensor_scalar(out=angsh[:], in0=ang[:, cs_], scalar1=shift,
                                                                          scalar2=None, op0=ALU.add))
                sop("dve", lambda e: e.tensor_scalar(out=kf[:], in0=angsh[:], scalar1=1.0 / (2.0 * PI), scalar2=None, op0=ALU.mult))
                sop("dve", lambda e: e.tensor_copy(out=ki[:], in_=kf[:]))
                sop("dve", lambda e: e.tensor_copy(out=kf[:], in_=ki[:]))
                sop("dve", lambda e: e.scalar_tensor_tensor(out=r1[:], in0=kf[:], scalar=-2.0 * PI, in1=angsh[:],
                                                            op0=ALU.mult, op1=ALU.add))
                sop("dve", lambda e: e.tensor_scalar(out=kf[:], in0=r1[:], scalar1=PI, scalar2=2.0 * PI, op0=ALU.is_gt, op1=ALU.mult))
                sop("dve", lambda e: e.tensor_tensor(out=r1[:], in0=r1[:], in1=kf[:], op=ALU.subtract))
                v = sop("dve", lambda e: e.tensor_scalar(out=r1[:], in0=r1[:], scalar1=-PI, scalar2=PI, op0=ALU.max, op1=ALU.min))
                P.wait("act", S["dve"], v)
                P.wait("act", S["tab"], S["tab"].n)
                va = sop("act", lambda e: e.activation(out=tabs[:], in_=r1[:], func=AF.Sin))
                P.wait("sp", S["act"], va)
                dma("sp", "tab", dst[s * 128:(s + 1) * 128, cs_], tabs[:])

    i8 = sb("i8", [128, 8], I32)
    f8 = sb("f8", [128, 8], F32)
    c8 = sb("c8", [128, 8], F32)
    sop("pool", lambda e: e.iota(i8[:], [[1, 8]], base=0, channel_multiplier=0))
    v_f8 = sop("pool", lambda e: e.tensor_copy(out=f8[:], in_=i8[:]))
    P.wait("dve", S["pool"], v_f8)
    for s in range(NSEQ):
        for g, w in enumerate(WINDOWS):
            o0 = (s * 4 + g) * 8
            sop("dve", lambda e, w=w: e.tensor_scalar(out=c8[:], in0=f8[:], scalar1=float(w // 2), scalar2=float(w),
                                                      op0=ALU.add, op1=ALU.min))
            sop("dve", lambda e: e.reciprocal(out=c8[:], in_=c8[:]))
            sop("dve", lambda e, w=w: e.tensor_scalar(out=c8[:], in0=c8[:], scalar1=-1.0 / w, scalar2=None, op0=ALU.add))
            sop("dve", lambda e, s=s: e.tensor_scalar(out=c8[:], in0=c8[:], scalar1=metas[:, 2 + 2 * s:3 + 2 * s],
                                                      scalar2=None, op0=ALU.mult))
            sop("dve", lambda e, w=w, o0=o0: e.tensor_scalar(out=invL[:, o0:o0 + 8], in0=c8[:], scalar1=1.0 / w,
                                                             scalar2=None, op0=ALU.add))
            sop("dve", lambda e, w=w: e.tensor_scalar(out=c8[:], in0=f8[:], scalar1=-1.0, scalar2=float(8 + w // 2),
                                                      op0=ALU.mult, op1=ALU.add))
            sop("dve", lambda e, w=w: e.tensor_scalar(out=c8[:], in0=c8[:], scalar1=float(w), scalar2=None, op0=ALU.min))
            sop("dve", lambda e: e.reciprocal(out=c8[:], in_=c8[:]))
            sop("dve", lambda e, w=w: e.tensor_scalar(out=c8[:], in0=c8[:], scalar1=-1.0 / w, scalar2=None, op0=ALU.add))
            sop("dve", lambda e, s=s: e.tensor_scalar(out=c8[:], in0=c8[:], scalar1=metas[:, 3 + 2 * s:4 + 2 * s],
                                                      scalar2=None, op0=ALU.mult))
            sop("dve", lambda e, w=w, o0=o0: e.tensor_scalar(out=invR[:, o0:o0 + 8], in0=c8[:], scalar1=1.0 / w,
                                                             scalar2=None, op0=ALU.add))

    cv = Carver()
    xt = TileV(cv("xt", [128, 4, D], F32))
    hb = TileV(cv("hb", [128, 4, D], BF16))
    hT = TileV(cv("hT", [128, 8, CH], BF16))
    ss = TileV(cv("ss", [128, 8], F32))
    rstd = TileV(cv("rstd", [128, 8], F32))
    cs_c = TileV(cv("cs_c", [128, CH], F32))
    cs_s = TileV(cv("cs_s", [128, CH], F32))
    t1 = TileV(cv("t1", [128, CH], F32))
    t2 = TileV(cv("t2", [128, CH], F32))
    ustg = TileV(cv("ustg", [128, 4, CH], F32))
    zpstg = TileV(cv("zpstg", [128, 4, CH], F32))
    zastg = TileV(cv("zastg", [128, 4, CH], BF16))
    qstg = TileV(cv("qstg", [128, 4, CH], BF16))
    kstg = TileV(cv("kstg", [128, 4, CH], BF16))
    vstg = TileV(cv("vstg", [128, 4, 4, VW], BF16))
    cv = Carver()
    QT = TileV(cv("QT", [128, T], BF16))
    ZA = TileV(cv("ZA", [128, T], BF16))
    AO = TileV(cv("AO", [128, T], BF16))
    Pb = [TileV(cv("P0", [128, 1024], BF16)), TileV(cv("P1", [128, 1024], BF16))]
    oc = TileV(cv("oc", [128, 3 * 512], F32))
    osb = TileV(cv("osb", [128, 4, 128], F32))
    o2s = TileV(cv("o2s", [128, 4, 128], F32))
    osq = TileV(cv("osq", [128, 128], F32))
    onb = TileV(cv("onb", [128, 4, 128], BF16))
    rc = TileV(cv("rc", [128, 16], F32))
    rc2 = TileV(cv("rc2", [128, 8], F32))
    cv = Carver()
    Uh = TileV(cv("Uh", [128, 4, CH + 16], F32))
    p2 = TileV(cv("p2", [128, CH + 16], F32))
    p4 = TileV(cv("p4", [128, CH + 16], F32))
    p8 = TileV(cv("p8", [128, CH + 16], F32))
    sw = TileV(cv("sw", [128, CH], F32))
    dif = TileV(cv("dif", [128, 4, CH], BF16))
    zpl = TileV(cv("zpl", [128, 4, CH], F32))
    catT = TileV(cv("catT", [128, 8, CH], BF16))
    edg = TileV(cv("edg", [128, NRANK, 4, 16], F32))
    HL = TileV(cv("HL", [128, 4, 8], F32))
    HR = TileV(cv("HR", [128, 4, 8], F32))
    yt = TileV(cv("yt", [128, D], F32))
    xo = TileV(cv("xo", [128, D], F32))
    xr = TileV(cv("xr", [128, 4, D], F32))

    PS = ps("PS", [128, 8 * 512], F32)

    def bank(i, n=1):
        return PS[:, i * 512:(i + n) * 512]

    def prep_weights(l):
        lam_init = 0.8 - 0.6 * math.exp(-0.3 * l)
        dma("sp", "wld", preg[:], pre_g[l].rearrange("(c p) -> p c", p=128), slow=True)
        dma("sp", "wld", pscale[:], pool_scale[l].rearrange("(g p) -> p g", p=128), slow=True)
        dma("sp", "wld", postg[:], post_g[l:l + 1, :].partition_broadcast(128))
        dma("sp", "wld", sg[:], subln_g[l].rearrange("(p o) -> p o", o=1))
        v_small = dma("sp", "wld", lamt[:], lam_in[l:l + 1, :].partition_broadcast(128))
        P.wait("dve", S["wld"], v_small)
        sop("dve", lambda e: e.tensor_scalar(out=sg[:], in0=sg[:], scalar1=float(1.0 - lam_init), scalar2=None, op0=ALU.mult))
        sop("dve", lambda e: e.tensor_scalar(out=npreg[:], in0=preg[:], scalar1=-1.0, scalar2=None, op0=ALU.mult))
        sop("dve", lambda e: e.tensor_tensor(out=lamp[:, 0:64], in0=lamt[:, 0:64], in1=lamt[:, 64:128], op=ALU.mult))
        sop("dve", lambda e: e.tensor_tensor(out=lamp[:, 64:128], in0=lamt[:, 128:192], in1=lamt[:, 192:256], op=ALU.mult))
        sop("dve", lambda e: e.reduce_sum(out=lam[:, 0:1], in_=lamp[:, 0:64], axis=AX.X))
        v = sop("dve", lambda e: e.reduce_sum(out=lam[:, 1:2], in_=lamp[:, 64:128], axis=AX.X))
        P.wait("act", S["dve"], v)
        v = sop("act", lambda e: e.activation(out=lam[:, 2:4], in_=lam[:, 0:2], func=AF.Exp))
        P.wait("dve", S["act"], v)
        sop("dve", lambda e: e.tensor_tensor(out=lam[:, 0:1], in0=lam[:, 2:3], in1=lam[:, 3:4], op=ALU.subtract))
        sop("dve", lambda e: e.tensor_scalar(out=lam[:, 0:1], in0=lam[:, 0:1], scalar1=float(lam_init), scalar2=None, op0=ALU.add))
        for dc in range(8):
            P.wait("sp", S["dve"], S["dve"].n)
            v = dma("sp", "wld", wstage[:], w_in[l, dc * 128:(dc + 1) * 128, :])
            P.wait("dve", S["wld"], v)
            sop("dve", lambda e, dc=dc: e.tensor_scalar(out=Wv[:, dc, 0:3072], in0=wstage[:], scalar1=preg[:, dc:dc + 1],
                                                        scalar2=None, op0=ALU.mult))
            src = wstage[:, 1024:2048].rearrange("p (b h i) -> p b h i", h=2, i=32)
            dst = Wv[:, dc, 3072:4096].rearrange("p (b h i) -> p b h i", h=2, i=32)
            sop("dve", lambda e, dc=dc, src=src, dst=dst: e.tensor_scalar(
                out=dst[:, :, 0, :], in0=src[:, :, 1, :], scalar1=npreg[:, dc:dc + 1], scalar2=None, op0=ALU.mult))
            sop("dve", lambda e, dc=dc, src=src, dst=dst: e.tensor_scalar(
                out=dst[:, :, 1, :], in0=src[:, :, 0, :], scalar1=preg[:, dc:dc + 1], scalar2=None, op0=ALU.mult))
        for mc in range(8):
            P.wait("sp", S["dve"], S["dve"].n)
            v = dma("sp", "wld", wstage[:, 0:D], w_out[l, mc * 128:(mc + 1) * 128, :])
            P.wait("dve", S["wld"], v)
            if mc < 4:
                sop("dve", lambda e, mc=mc: e.tensor_copy(out=Wo[:, mc, :], in_=wstage[:, 0:D]))
            else:
                sop("dve", lambda e, mc=mc: e.tensor_scalar(out=Wo[:, mc, :], in0=wstage[:, 0:D], scalar1=sg[:, 0:1],
                                                            scalar2=None, op0=ALU.mult))
        P.wait("sp", S["dve"], S["dve"].n)
        v = dma("sp", "wld", wstage[:, 0:512].rearrange("p (g d) -> p g d", g=4), pool_w[l].rearrange("g c d -> c g d"))
        P.wait("dve", S["wld"], v)
        v = dma("sp", "wld", wstage[:, 512:1024], pool_scale[l:l + 1, :].partition_broadcast(128))
        P.wait("dve", S["wld"], v)
        sop("dve", lambda e: e.tensor_tensor(out=PW[:].rearrange("p g d -> p (g d)"), in0=wstage[:, 0:512], in1=wstage[:, 512:1024], op=ALU.mult))


    pj_state = {"i": 0, "free": {}}

    def phase1_chunk(l, s, ci, xsrc):
        r0 = s * T + ci * CH
        c0 = ci * CH
        wait_all("sp", ["dve", "act", "pj", "tp"])
        dma("sp", "xin", xt[:], xsrc[r0:r0 + CH, :].rearrange("(j p) d -> p j d", p=128))
        dma("sp", "xin", cs_c[:], cosT[s * 128:(s + 1) * 128, c0:c0 + CH])
        v_in = dma("sp", "xin", cs_s[:], sinT[s * 128:(s + 1) * 128, c0:c0 + CH])
        P.wait("act", S["xin"], v_in)
        P.wait("dve", S["xin"], v_in)
        P.wait("act", S["pj"], S["pj"].n)
        P.wait("dve", S["pj"], S["pj"].n)
        P.wait("dve", S["tp"], S["tp"].n)
        sop("act", lambda e: e.memzero(ss[:, 0:4]))
        for j in range(4):
            va = sop("act", lambda e, j=j: e.activation(out=scr[:, 0:D], in_=xt[:, j, :], func=AF.Square,
                                                        accum_out=ss[:, j:j + 1]))
        sop("act", lambda e: e.activation(out=rstd[:, 0:4], in_=ss[:, 0:4], func=AF.Ln, scale=1.0 / D, bias=epsn[:, 0:1]))
        va = sop("act", lambda e: e.activation(out=rstd[:, 0:4], in_=rstd[:, 0:4], func=AF.Exp, scale=-0.5))
        P.wait("dve", S["act"], va)
        for j in range(4):
            vh = sop("dve", lambda e, j=j: e.tensor_scalar(out=hb[:, j, :], in0=xt[:, j, :], scalar1=rstd[:, j:j + 1],
                                                           scalar2=None, op0=ALU.mult))
        LEVEL = int(os.environ.get("P1_LEVEL", 9))
        if LEVEL < 2:
            return
        P.wait("pe", S["dve"], vh)
        for dc in range(8):
            half = dc % 2
            tpv = bank(half)
            cs = S["tpc_d"] if dc % 2 == 0 else S["tpc_a"]
            P.wait("pe", cs, cs.n)
            for j in range(4):
                vt = P.op("pe", lambda e, j=j, dc=dc, tpv=tpv: e.matmul(
                    tpv[:, j * 128:(j + 1) * 128], lhsT=hb[:, j, dc * 128:(dc + 1) * 128], rhs=ident[:],
                    start=True, stop=True), S["tp"])
            if dc % 2 == 0:
                P.wait("dve", S["tp"], vt)
                P.op("dve", lambda e, dc=dc, tpv=tpv: e.tensor_copy(out=hT[:, dc, :], in_=tpv), cs)
            else:
                P.wait("act", S["tp"], vt)
                P.op("act", lambda e, dc=dc, tpv=tpv: e.copy(out=hT[:, dc, :], in_=tpv), cs)
        P.wait("pe", S["tpc_d"], S["tpc_d"].n)
        P.wait("pe", S["tpc_a"], S["tpc_a"].n)

        if LEVEL < 3:
            return
        def proj(fc, bk):
            for dc in range(8):
                ins = lambda e, dc=dc: e.matmul(bank(bk), lhsT=Wv[:, dc, fc * 128:(fc + 1) * 128], rhs=hT[:, dc, :],
                                                start=(dc == 0), stop=(dc == 7))
                if dc == 7:
                    return P.op("pe", ins, S["pj"])
                P.op("pe", ins)

        def next_bank(n=1):
            i = pj_state["i"]
            if n == 2 and i % 2 == 1:
                i += 1
            bk = 2 + (i % 6)
            pj_state["i"] = i + n
            return bk

        free_hist = pj_state["free"]

        def acquire(bk):
            v = free_hist.get(bk)
            if v is not None:
                P.wait("pe", S[v[0]], v[1])

        wait_all("dve", ["stg"])
        wait_all("act", ["stg"])

        for g in range(4):
            bk = next_bank()
            acquire(bk)
            vp = proj(g, bk)
            P.wait("act", S["pj"], vp)
            free_hist[bk] = ("pjf_a", P.op("act", lambda e, g=g, bk=bk: e.copy(out=ustg[:, g, :], in_=bank(bk)), S["pjf_a"]))
        if LEVEL < 4:
            return
        for (fc0, stg) in ((4, zpstg), (20, zastg)):
            for g in range(4):
                bk = next_bank()
                acquire(bk)
                vp = proj(fc0 + g, bk)
                P.wait("act", S["pj"], vp)
                free_hist[bk] = ("pjf_a", P.op("act", lambda e, g=g, bk=bk, stg=stg: e.activation(
                    out=stg[:, g, :], in_=bank(bk), func=AF.Silu), S["pjf_a"]))
        if LEVEL < 5:
            return
        for (fc0, fr0, stg) in ((8, 24, qstg), (12, 28, kstg)):
            for h in range(4):
                bk = next_bank(2)
                acquire(bk)
                acquire(bk + 1)
                proj(fc0 + h, bk)
                vp = proj(fr0 + h, bk + 1)
                P.wait("dve", S["pj"], vp)
                sop("dve", lambda e, bk=bk: e.tensor_tensor(out=t1[:], in0=bank(bk), in1=cs_c[:], op=ALU.mult))
                vf = P.op("dve", lambda e, bk=bk: e.tensor_tensor(out=t2[:], in0=bank(bk + 1), in1=cs_s[:], op=ALU.mult), S["pjf_d"])
                free_hist[bk] = ("pjf_d", vf)
                free_hist[bk + 1] = ("pjf_d", vf)
                P.wait("dve", S["pjf_d"], vf)
                sop("dve", lambda e, h=h, stg=stg: e.tensor_tensor(out=stg[:, h, :], in0=t1[:], in1=t2[:], op=ALU.add))
        v_qk = S["dve"].n
        if LEVEL < 6:
            return
        for j in range(4):
            bk = next_bank()
            acquire(bk)
            for dc in range(8):
                ins = lambda e, dc=dc, j=j, bk=bk: e.matmul(bank(bk), lhsT=hT[:, dc, j * 128:(j + 1) * 128],
                                                            rhs=Wv[:, dc, 2048:2560], start=(dc == 0), stop=(dc == 7))
                if dc == 7:
                    vp = P.op("pe", ins, S["pj"])
                else:
                    P.op("pe", ins)
            P.wait("act", S["pj"], vp)
            free_hist[bk] = ("pjf_a", P.op("act", lambda e, j=j, bk=bk: e.copy(
                out=vstg[:, :, j, 0:128], in_=bank(bk).rearrange("p (h e) -> p h e", h=4)), S["pjf_a"]))
        v_v = S["pjf_a"].n

        if os.environ.get("P1_NOSTORE"):
            return
        P.wait("sp", S["pjf_a"], v_v)
        P.wait("sp", S["dve"], v_qk)
        rows = lambda base: base.rearrange("(g p) t -> p g t", p=128)
        dma("sp", "stg", rows(uT_d[s * 512:(s + 1) * 512, c0:c0 + CH]), ustg[:])
        dma("sp", "stg", rows(zpT_d[s * 512:(s + 1) * 512, c0:c0 + CH]), zpstg[:])
        dma("sp", "stg", rows(zaT_d[s * 512:(s + 1) * 512, c0:c0 + CH]), zastg[:])
        dma("sp", "stg", rows(qT_d[s * 512:(s + 1) * 512, c0:c0 + CH]), qstg[:])
        if s == 0:
            dma("sp", "stg", rows(kTA_d[:, c0:c0 + CH]), kstg[:])
            dma("sp", "stg", vA_d.rearrange("p (h k w) -> p h k w", h=4, w=VW)[:, :, ci * 4:(ci + 1) * 4, :], vstg[:])
        else:
            for h in range(4):
                dma("sp", "stg", kTl_h[h][:, c0:c0 + CH], kstg[:, h, :])
                k0 = (ci % 4) * 4
                dma("sp", "stg", vl_h[h][ci // 4].rearrange("p (k w) -> p k w", w=VW)[:, k0:k0 + 4, :], vstg[:, h, :, :])
        if s == 1 and ci == 0:
            dma("sp", "stg", rows(uel_d[:, 0:8]), ustg[:, :, 0:8])
        if s == 1 and ci == NCH - 1:
            dma("sp", "stg", rows(uel_d[:, 8:16]), ustg[:, :, CH - 8:CH])


    def oacc(c, j):
        idx = c * 4 + j
        bk, slot = divmod(idx, 3)
        off = (4 + bk) * 512 + slot * VW
        return PS[:, off:off + VW]

    def ocv(c, j):
        idx = c * 4 + j
        bk, slot = divmod(idx, 3)
        off = bk * 512 + slot * VW
        return oc[:, off:off + VW]

    TP7 = bank(7)
    KVS = ("kv0", "kv1", "kv2", "kv3")

    def attention(l, s, h):
        nk = T if s == 0 else NRANK * T
        nkt = nk // 128
        wait_all("sp", ["av_done", "epf", "ost"])
        vkv = []
        if s == 0:
            dma("sp", "kv0", KT[:, 0:T], kTA_d[h * 128:(h + 1) * 128, :])
            vkv.append(dma("sp", "kv0", VV[:, 0:32, :], vA_d[:, h * HV:(h + 1) * HV].rearrange("p (k w) -> p k w", w=VW)))
        else:
            for r in range(NRANK):
                dma("sp", KVS[r], KT[:, r * T:(r + 1) * T], kTall_h[h][r * 128:(r + 1) * 128, :])
                for f in range(2):
                    vlast = dma("sp", KVS[r], VV[:, r * 32 + f * 16:r * 32 + (f + 1) * 16, :],
                                vall_h[h][f][r * 128:(r + 1) * 128, :].rearrange("p (k w) -> p k w", w=VW))
                vkv.append(vlast)
        dma("sp", "qld", QT[:], qT_d[s * 512 + h * 128:s * 512 + (h + 1) * 128, :])
        v_q = dma("sp", "qld", ZA[:], zaT_d[s * 512 + h * 128:s * 512 + (h + 1) * 128, :])
        P.wait("pe", S["qld"], v_q)
        P.wait("dve", S["qld"], v_q)
        pendB = []
        pendC = []

        for qc in range(T // 512):
            q0 = qc * 512
            sr0 = S["s_ready"].n
            ex0 = S["exp_done"].n
            av0 = S["av_done"].n
            v_accfree = S["accfree"].n

            def qk(kt):
                b = kt % 2
                if qc == 0:
                    r = kt // 32
                    P.wait("pe", S[KVS[r]], vkv[r])
                if kt >= 2:
                    P.wait("pe", S["exp_done"], ex0 + kt - 1)
                for c in range(2):
                    ins = lambda e, c=c, b=b, kt=kt, q0=q0: e.matmul(
                        PS[:, (2 * b + c) * 512:(2 * b + c + 1) * 512],
                        lhsT=KT[c * 64:(c + 1) * 64, kt * 128:(kt + 1) * 128],
                        rhs=QT[c * 64:(c + 1) * 64, q0:q0 + 512], start=True, stop=True)
                    if c == 1:
                        P.op("pe", ins, S["s_ready"])
                    else:
                        P.op("pe", ins)

            def av(kt):
                b = kt % 2
                if kt == 0:
                    P.wait("pe", S["accfree"], v_accfree)
                P.wait("pe", S["exp_done"], ex0 + kt + 1)
                for c in range(2):
                    for j in range(4):
                        idx = c * 4 + j
                        ins = lambda e, c=c, j=j, b=b, kt=kt, idx=idx: e.matmul(
                            oacc(c, j), lhsT=Pb[b][:, c * 512 + j * 128:c * 512 + (j + 1) * 128], rhs=VV[:, kt, :],
                            start=(kt == 0 and idx % 3 == 0), stop=(kt == nkt - 1), skip_group_check=True)
                        if idx == 7:
                            P.op("pe", ins, S["av_done"])
                        else:
                            P.op("pe", ins)

            def ex(kt):
                b = kt % 2
                P.wait("act", S["s_ready"], sr0 + kt + 1)
                if kt >= 2:
                    P.wait("act", S["av_done"], av0 + kt - 1)
                P.op("act", lambda e, b=b: e.activation(out=Pb[b][:, :], in_=PS[:, 2 * b * 512:(2 * b + 2) * 512],
                                                        func=AF.Exp, scale=0.125), S["exp_done"])

            P.wait("act", S["av_done"], av0)
            qk(0)
            ex(0)
            for kt in range(nkt):
                if kt + 1 < nkt:
                    qk(kt + 1)
                    ex(kt + 1)
                av(kt)
                if kt == 4 and pendB:
                    pendB.pop()()
                if kt == 12 and pendC:
                    pendC.pop()()

            P.wait("dve", S["av_done"], av0 + nkt)
            for b3 in range(3):
                if b3 < 2:
                    P.op("dve", lambda e, b3=b3: e.tensor_copy(out=oc[:, b3 * 512:(b3 + 1) * 512], in_=bank(4 + b3)))
                else:
                    vaf = P.op("dve", lambda e, b3=b3: e.tensor_copy(out=oc[:, b3 * 512:(b3 + 1) * 512], in_=bank(4 + b3)), S["accfree"])
            P.wait("dve", S["accfree"], vaf)
            if debug and s == 0 and h == 0 and qc == 0:
                P.wait("sp", S["accfree"], vaf)
                dma("sp", "ost", dbg_oc, oc[:])
                dma("sp", "ost", dbg_p, Pb[1][:])
                P.wait("dve", S["ost"], S["ost"].n)
                P.wait("act", S["ost"], S["ost"].n)
            P.wait("dve", S["tp"], S["tp"].n)
            for j in range(4):
                O1, O2 = ocv(0, j), ocv(1, j)
                sop("dve", lambda e, O1=O1, j=j: e.reciprocal(out=rc[:, 4 * j:4 * j + 1], in_=O1[:, 128:129]))
                sop("dve", lambda e, O2=O2, j=j: e.reciprocal(out=rc[:, 4 * j + 1:4 * j + 2], in_=O2[:, 128:129]))
                sop("dve", lambda e, j=j: e.tensor_tensor(out=rc[:, 4 * j + 1:4 * j + 2], in0=rc[:, 4 * j + 1:4 * j + 2],
                                                          in1=lam[:, 0:1], op=ALU.mult))
                sop("dve", lambda e, O2=O2, j=j: e.tensor_scalar(out=o2s[:, j, :], in0=O2[:, 0:128],
                                                                 scalar1=rc[:, 4 * j + 1:4 * j + 2], scalar2=None, op0=ALU.mult))
                sop("dve", lambda e, O1=O1, j=j: e.scalar_tensor_tensor(
                    out=osb[:, j, :], in0=O1[:, 0:128], scalar=rc[:, 4 * j:4 * j + 1], in1=o2s[:, j, :],
                    op0=ALU.mult, op1=ALU.subtract))
                sop("dve", lambda e, j=j: e.tensor_tensor(out=osq[:], in0=osb[:, j, :], in1=osb[:, j, :], op=ALU.mult))
                sop("dve", lambda e, j=j: e.reduce_sum(out=rc2[:, j:j + 1], in_=osq[:], axis=AX.X))
            vss = S["dve"].n

            def finB(vss=vss):
                P.wait("act", S["dve"], vss)
                sop("act", lambda e: e.activation(out=rc2[:, 0:4], in_=rc2[:, 0:4], func=AF.Ln, scale=1.0 / 128, bias=epsn[:, 1:2]))
                va = sop("act", lambda e: e.activation(out=rc2[:, 0:4], in_=rc2[:, 0:4], func=AF.Exp, scale=-0.5))
                P.wait("dve", S["act"], va)
                for j in range(4):
                    P.op("dve", lambda e, j=j: e.tensor_scalar(out=onb[:, j, :], in0=osb[:, j, :],
                                                               scalar1=rc2[:, j:j + 1], scalar2=None, op0=ALU.mult), S["ep"])

            def finC(q0=q0):
                P.wait("pe", S["ep"], S["ep"].n)
                P.wait("pe", S["epf"], S["epf"].n)
                for j in range(4):
                    vt = P.op("pe", lambda e, j=j: e.matmul(TP7[:, j * 128:(j + 1) * 128], lhsT=onb[:, j, :], rhs=ident[:],
                                                            start=True, stop=True), S["tp"])
                P.wait("dve", S["tp"], vt)
                P.op("dve", lambda e: e.tensor_tensor(out=AO[:, q0:q0 + 512], in0=TP7[:, 0:512], in1=ZA[:, q0:q0 + 512],
                                                      op=ALU.mult), S["epf"])
            if qc == T // 512 - 1:
                finB()
                finC()
            else:
                pendB.append(finB)
                pendC.append(finC)
        P.wait("sp", S["epf"], S["epf"].n)
        dma("sp", "ost", aoT_d[s * 512 + h * 128:s * 512 + (h + 1) * 128, :], AO[:])

    def load_edges():
        for r in range(NRANK):
            v = dma("sp", "pld", edg[:, r, :, :], ueall_d[r * 512:(r + 1) * 512, :].rearrange("(g p) c -> p g c", p=128))
        P.wait("dve", S["pld"], v)
        for (Ht, wc, c0) in ((HL, 8, 8), (HR, 12, 0)):
            sop("dve", lambda e, Ht=Ht, wc=wc, c0=c0: e.tensor_scalar(out=Ht[:], in0=edg[:, 0, :, c0:c0 + 8],
                                                                      scalar1=metas[:, wc:wc + 1], scalar2=None, op0=ALU.mult))
            for r in range(1, NRANK):
                sop("dve", lambda e, Ht=Ht, wc=wc, c0=c0, r=r: e.scalar_tensor_tensor(
                    out=Ht[:], in0=edg[:, r, :, c0:c0 + 8], scalar=metas[:, wc + r:wc + r + 1], in1=Ht[:],
                    op0=ALU.mult, op1=ALU.add))

    def phase5_chunk(l, s, ci, xsrc, ydst):
        r0 = s * T + ci * CH
        c0 = ci * CH
        wait_all("sp", ["dve", "act", "pj", "pool"])
        urows = uT_d[s * 512:(s + 1) * 512, :].rearrange("(g p) t -> p g t", p=128)
        lo = c0 - 8 if ci > 0 else c0
        hi = c0 + CH + 8 if ci < NCH - 1 else c0 + CH
        dma("sp", "pld", Uh[:, :, 8 - (c0 - lo):8 + CH + (hi - c0 - CH)], urows[:, :, lo:hi])
        dma("sp", "pld", zpl[:], zpT_d[s * 512:(s + 1) * 512, c0:c0 + CH].rearrange("(g p) t -> p g t", p=128))
        dma("sp", "pld", catT[:, 4:8, :], aoT_d[s * 512:(s + 1) * 512, c0:c0 + CH].rearrange("(g p) t -> p g t", p=128))
        v_ld = dma("sp", "pld", xr[:], xsrc[r0:r0 + CH, :].rearrange("(j p) d -> p j d", p=128))
        P.wait("dve", S["pld"], v_ld)
        P.wait("dve", S["pld"], v_ld)
        P.wait("pe", S["pld"], v_ld)
        if ci == 0:
            if s == 0:
                sop("dve", lambda e: e.memset(Uh[:, :, 0:8], 0.0))
            else:
                P.wait("dve", S["dve"], S["dve"].n)
                sop("dve", lambda e: e.tensor_copy(out=Uh[:, :, 0:8], in_=HL[:]))
        if ci == NCH - 1:
            if s == 0:
                sop("dve", lambda e: e.memset(Uh[:, :, CH + 8:CH + 16], 0.0))
            else:
                P.wait("dve", S["dve"], S["dve"].n)
                sop("dve", lambda e: e.tensor_copy(out=Uh[:, :, CH + 8:CH + 16], in_=HR[:]))
        L5 = int(os.environ.get("P5_LEVEL", 9))
        if L5 < 1:
            return
        N = CH + 16
        for g, w in enumerate(WINDOWS):
            u = Uh[:, g, :]
            if w == 2:
                sop("dve", lambda e, u=u: e.tensor_tensor(out=sw[:], in0=u[:, 7:7 + CH], in1=u[:, 8:8 + CH], op=ALU.add))
            else:
                sop("dve", lambda e, u=u: e.tensor_tensor(out=p2[:, 0:N - 1], in0=u[:, 0:N - 1], in1=u[:, 1:N], op=ALU.add))
                if w == 4:
                    sop("dve", lambda e: e.tensor_tensor(out=sw[:], in0=p2[:, 6:6 + CH], in1=p2[:, 8:8 + CH], op=ALU.add))
                else:
                    sop("dve", lambda e: e.tensor_tensor(out=p4[:, 0:N - 3], in0=p2[:, 0:N - 3], in1=p2[:, 2:N - 1], op=ALU.add))
                    if w == 8:
                        sop("dve", lambda e: e.tensor_tensor(out=sw[:], in0=p4[:, 4:4 + CH], in1=p4[:, 8:8 + CH], op=ALU.add))
                    else:
                        sop("dve", lambda e: e.tensor_tensor(out=p8[:, 0:N - 7], in0=p4[:, 0:N - 7], in1=p4[:, 4:N - 3], op=ALU.add))
                        sop("dve", lambda e: e.tensor_tensor(out=sw[:], in0=p8[:, 0:CH], in1=p8[:, 8:8 + CH], op=ALU.add))
            sop("dve", lambda e, w=w: e.tensor_scalar(out=sw[:], in0=sw[:], scalar1=1.0 / w, scalar2=None, op0=ALU.mult))
            o0 = (s * 4 + g) * 8
            if ci == 0:
                sop("dve", lambda e, w=w: e.tensor_scalar(out=sw[:, 0:8], in0=sw[:, 0:8], scalar1=float(w), scalar2=None, op0=ALU.mult))
                sop("dve", lambda e, o0=o0: e.tensor_tensor(out=sw[:, 0:8], in0=sw[:, 0:8], in1=invL[:, o0:o0 + 8], op=ALU.mult))
            if ci == NCH - 1:
                sop("dve", lambda e, w=w: e.tensor_scalar(out=sw[:, CH - 8:CH], in0=sw[:, CH - 8:CH], scalar1=float(w), scalar2=None, op0=ALU.mult))
                sop("dve", lambda e, o0=o0: e.tensor_tensor(out=sw[:, CH - 8:CH], in0=sw[:, CH - 8:CH], in1=invR[:, o0:o0 + 8], op=ALU.mult))
            vd = sop("dve", lambda e, u=u, g=g: e.tensor_tensor(out=dif[:, g, :], in0=sw[:], in1=u[:, 8:8 + CH], op=ALU.subtract))
        if L5 < 2:
            return
        P.wait("pe", S["dve"], vd)
        P.wait("pe", S["dve"], S["dve"].n)
        for g in range(4):
            vp = P.op("pe", lambda e, g=g: e.matmul(bank(g), lhsT=PW[:, g, :], rhs=dif[:, g, :], start=True, stop=True), S["pj"])
            P.wait("dve", S["pj"], vp)
            sop("dve", lambda e, g=g: e.tensor_tensor(out=catT[:, g, :], in0=bank(g), in1=zpl[:, g, :], op=ALU.mult))
        v_cat = S["dve"].n
        if L5 < 3:
            return
        P.wait("pe", S["dve"], v_cat)
        for j in range(4):
            yb = 4 + 2 * (j % 2)
            P.wait("pe", S["pjf_d"], pj_state.get(("y", j % 2), 0))
            for half in range(2):
                for mc in range(8):
                    ins = lambda e, mc=mc, half=half, j=j, yb=yb: e.matmul(
                        bank(yb + half), lhsT=catT[:, mc, j * 128:(j + 1) * 128], rhs=Wo[:, mc, half * 512:(half + 1) * 512],
                        start=(mc == 0), stop=(mc == 7))
                    if mc == 7 and half == 1:
                        vy = P.op("pe", ins, S["pj"])
                    else:
                        P.op("pe", ins)
            Y = PS[:, yb * 512:(yb + 2) * 512]
            P.wait("dve", S["pj"], vy)
            P.wait("dve", S["act"], S["act"].n)
            P.op("dve", lambda e, Y=Y: e.tensor_copy(out=yt[:, 0:512], in_=Y[:, 0:512]))
            vf = P.op("dve", lambda e, Y=Y: e.tensor_copy(out=yt[:, 512:1024], in_=Y[:, 512:1024]), S["pjf_d"])
            pj_state[("y", j % 2)] = vf
            P.wait("dve", S["pjf_d"], vf)
            sop("dve", lambda e: e.tensor_tensor(out=scr[:, 0:D], in0=yt[:], in1=yt[:], op=ALU.mult))
            vs_ = sop("dve", lambda e: e.reduce_sum(out=ss5[:, 0:1], in_=scr[:, 0:D], axis=AX.X))
            P.wait("act", S["dve"], vs_)
            sop("act", lambda e: e.activation(out=rstd5[:, 0:1], in_=ss5[:, 0:1], func=AF.Ln, scale=1.0 / D, bias=epsn[:, 0:1]))
            va = sop("act", lambda e: e.activation(out=rstd5[:, 0:1], in_=rstd5[:, 0:1], func=AF.Exp, scale=-0.5))
            P.wait("dve", S["act"], va)
            sop("dve", lambda e: e.tensor_scalar(out=yt[:], in0=yt[:], scalar1=rstd5[:, 0:1], scalar2=None, op0=ALU.mult))
            sop("dve", lambda e: e.tensor_tensor(out=yt[:], in0=yt[:], in1=postg[:], op=ALU.mult))
            P.wait("dve", S["ost"], S["ost"].n)
            vo = sop("dve", lambda e, j=j: e.tensor_tensor(out=xo[:], in0=yt[:], in1=xr[:, j, :], op=ALU.add))
            P.wait("sp", S["dve"], vo)
            dma("sp", "ost", ydst[r0 + j * 128:r0 + (j + 1) * 128, :], xo[:])

    wait_all("sp", ["tab"])
    barrier()

    groups = [[0, 1, 2, 3], [4, 5, 6, 7]]
    for l in range(nlayers):
        xsrc = x_in if l == 0 else x1
        ydst = x1 if l < nlayers - 1 else y_out
        if stop_after == "setup":
            break
        prep_weights(l)
        barrier()
        if stop_after == "prep":
            break
        sop("pool", lambda e: e.memset(vstg[:, :, :, 128:129], 1.0))
        sop("pool", lambda e: e.memset(vstg[:, :, :, 129:VW], 0.0))
        barrier()
        for s in (1, 0):
            for ci in range(int(os.environ.get("P1_MAXCH", NCH))):
                phase1_chunk(l, s, ci, xsrc)
            if s == 1 and stop_after != "p1nc":
                wait_all("sp", ["stg"])
                barrier()
                for h in range(4):
                    P.op("pool", lambda e, h=h: e.collective_compute("AllGather", ALU.bypass, replica_groups=groups,
                                                                     ins=[kTl_h[h]], outs=[kTall_h[h]]), S["cc"])
                    for f in range(2):
                        P.op("pool", lambda e, h=h, f=f: e.collective_compute("AllGather", ALU.bypass, replica_groups=groups,
                                                                              ins=[vl_h[h][f]], outs=[vall_h[h][f]]), S["cc"])
                P.op("pool", lambda e: e.collective_compute("AllGather", ALU.bypass, replica_groups=groups,
                                                            ins=[uel_d], outs=[ueall_d]), S["cc"])
                for e_ in ENGS:
                    P.wait(e_, S["cc"], S["cc"].n)
                barrier()
        if stop_after in ("p1", "p1nc"):
            break
        wait_all("sp", ["stg"])
        barrier()
        for e_ in ENGS:
            P.wait(e_, S["cc"], S["cc"].n)
        for s in (0, 1):
            for h in range(4):
                attention(l, s, h)
        wait_all("sp", ["ost"])
        if stop_after == "att":
            break
        barrier()
        load_edges()
        for s in range(NSEQ):
            for ci in range(int(os.environ.get("P5_MAXCH", NCH))):
                phase5_chunk(l, s, ci, xsrc, ydst)
        wait_all("sp", ["ost"])
        barrier()

    for n in DMA_SEMS:
        P.wait("sp", S[n], S[n].n)
    barrier()

    with nc.Block() as block:
        @block.tensor
        def _(e):
            for f in P.q["pe"]:
                f(e)

        @block.scalar
        def _(e):
            for f in P.q["act"]:
                f(e)

        @block.vector
        def _(e):
            for f in P.q["dve"]:
                f(e)

        @block.gpsimd
        def _(e):
            for f in P.q["pool"]:
                f(e)

        @block.sync
        def _(e):
            for f in P.q["sp"]:
                f(e)
    es.close()
    return nc


def make_in_maps(inputs):
    x_prompt = np.asarray(inputs["x_prompt"], np.float32)
    x_sample = np.asarray(inputs["x_sample"], np.float32)
    lam_in = np.concatenate([np.asarray(inputs[k], np.float32) for k in
                             ("lambda_q1", "lambda_k1", "lambda_q2", "lambda_k2")], axis=1)
    common = {
        "w_in": np.ascontiguousarray(inputs["w_in"], dtype=np.float32),
        "w_out": np.ascontiguousarray(inputs["w_out"], dtype=np.float32),
        "pool_w": np.ascontiguousarray(inputs["pool_w"], dtype=np.float32),
        "pool_scale": np.ascontiguousarray(inputs["pool_scale"], dtype=np.float32),
        "pre_g": np.ascontiguousarray(inputs["pre_norm_g"], dtype=np.float32),
        "post_g": np.ascontiguousarray(inputs["post_norm_g"], dtype=np.float32),
        "subln_g": np.ascontiguousarray(inputs["subln_g"], dtype=np.float32),
        "lam_in": np.ascontiguousarray(lam_in),
    }
    maps = []
    for c in range(NCORES):
        b, r = divmod(c, NRANK)
        xin = np.concatenate([x_sample[c], x_prompt[b, r * T:(r + 1) * T]], axis=0)
        meta = np.zeros((128, 16), np.float32)
        meta[:, 0] = 0.0
        meta[:, 1] = float(r * T)
        meta[:, 2] = 1.0
        meta[:, 3] = 1.0
        meta[:, 4] = 1.0 if r == 0 else 0.0
        meta[:, 5] = 1.0 if r == NRANK - 1 else 0.0
        if r > 0:
            meta[:, 8 + r - 1] = 1.0
        if r < NRANK - 1:
            meta[:, 12 + r + 1] = 1.0
        m = dict(common)
        m["x_in"] = np.ascontiguousarray(xin)
        m["meta"] = meta
        maps.append(m)
    return maps


def kernel(**inputs):
    nc = build()
    maps = make_in_maps(inputs)
    res = run_bass_kernel_spmd(nc, maps, core_ids=list(range(NCORES)))
    ys = [r["y_out"] for r in res.results]
    y_sample = np.stack([ys[c][0:T] for c in range(NCORES)], axis=0)
    y_prompt = np.stack([np.concatenate([ys[b * NRANK + r][T:2 * T] for r in range(NRANK)], axis=0)
                         for b in range(2)], axis=0)
    return (y_prompt.astype(np.float32), y_sample.astype(np.float32))
```
